# Optimizing a Trainium2 kernel written in Bass

```python
import jax, jax.numpy as jnp
from jax import lax
import numpy as np

D_MODEL = 1024
BATCH = 2
SEQ = 16384
DEPTH = 2

CHUNK = 64
PLE_DIM = 256
BRANCH_WIDTH = D_MODEL // 2
N_BRANCH = 4
SG_BLOCK = 128
SG_GROUPS = 4
SG_WIDTH = BRANCH_WIDTH
GLA_HEADS = 4
GLA_DK = 64
GLA_DV = BRANCH_WIDTH // GLA_HEADS
GLA_RANK = 16
GLA_TAU = 16.0
ATT_HEADS = 8
ATT_HD = BRANCH_WIDTH // ATT_HEADS
ATT_BAND = 9
MAX_REL = 256
REL_TABLE = CHUNK + MAX_REL
CONV_WIDTH = BRANCH_WIDTH
CONV_K = 31
D_FF = 4 * D_MODEL
EPS = 1e-6
NEG_INF = -1e30

IN_SPLITS = (SG_WIDTH, SG_WIDTH,
             GLA_HEADS * GLA_DK, GLA_HEADS * GLA_DK, GLA_HEADS * GLA_DV, GLA_HEADS * GLA_DV, GLA_RANK,
             ATT_HEADS * ATT_HD, ATT_HEADS * ATT_HD, ATT_HEADS * ATT_HD,
             CONV_WIDTH, CONV_WIDTH)
IN_COLS = 2 * SG_WIDTH + 2 * GLA_HEADS * GLA_DK + 2 * GLA_HEADS * GLA_DV + GLA_RANK + 3 * ATT_HEADS * ATT_HD + 2 * CONV_WIDTH

kernel_name = "hybrid_gated_branch_streaming_encoder"


def rms_norm(x, g):
    xf = x.astype(jnp.float32)
    y = xf * lax.rsqrt(jnp.mean(xf * xf, axis=-1, keepdims=True) + EPS)
    return (y * g.astype(jnp.float32)).astype(x.dtype)


def layer_norm(x, g, b):
    xf = x.astype(jnp.float32)
    mu = jnp.mean(xf, axis=-1, keepdims=True)
    xc = xf - mu
    y = xc * lax.rsqrt(jnp.mean(xc * xc, axis=-1, keepdims=True) + EPS)
    return (y * g.astype(jnp.float32) + b.astype(jnp.float32)).astype(x.dtype)


def spatial_gating(u, v, ln_g, ln_b, w_s, b_s):
    bsz, s, _ = u.shape
    nb = s // SG_BLOCK
    cg = SG_WIDTH // SG_GROUPS
    v = layer_norm(v, ln_g, ln_b)
    vb = v.reshape(bsz, nb, SG_BLOCK, SG_GROUPS, cg)
    pos = jnp.arange(SG_BLOCK)
    mask = (pos[None, :] // CHUNK) <= (pos[:, None] // CHUNK)
    w = jnp.where(mask[None], w_s, 0.0)
    mixed = jnp.einsum('gij,bnjgc->bnigc', w, vb) + b_s.T[None, None, :, :, None]
    return u * mixed.reshape(bsz, s, SG_WIDTH)


def gated_linear_attention(q, k, v, r, a_lr, w_a2, b_a, norm_g):
    f32 = jnp.float32
    bsz, s, _ = q.shape
    nc = s // CHUNK
    qc = q.astype(f32).reshape(bsz, nc, CHUNK, GLA_HEADS, GLA_DK) * (GLA_DK ** -0.5)
    kc = k.astype(f32).reshape(bsz, nc, CHUNK, GLA_HEADS, GLA_DK)
    vc = v.astype(f32).reshape(bsz, nc, CHUNK, GLA_HEADS, GLA_DV)
    log_a = jax.nn.log_sigmoid(jnp.einsum('bsr,rk->bsk', a_lr.astype(f32), w_a2.astype(f32))
                               + b_a.astype(f32)) / GLA_TAU
    log_a = log_a.reshape(bsz, nc, CHUNK, GLA_HEADS, GLA_DK)
    cum = jnp.cumsum(log_a, axis=2)
    total = cum[:, :, -1]
    k_dec = kc * jnp.exp(total[:, :, None] - cum)
    upd = jnp.einsum('bclhd,bclhe->cbhde', k_dec, vc)
    decay = jnp.exp(total).transpose(1, 0, 2, 3)

    def step(state, inp):
        d, u_c = inp
        state = d[..., None] * state + u_c
        return state, state

    s0 = jnp.zeros((bsz, GLA_HEADS, GLA_DK, GLA_DV), f32)
    _, states = lax.scan(step, s0, (decay, upd))
    o = jnp.einsum('bclhd,cbhde->bclhe', qc, states)
    o = o * lax.rsqrt(jnp.mean(o * o, axis=-1, keepdims=True) + EPS)
    o = o.reshape(bsz, s, GLA_HEADS * GLA_DV) * norm_g.astype(f32)
    return (o * jax.nn.silu(r.astype(f32))).astype(r.dtype)


def band_chunk_attention(q, k, v, rel_bias):
    f32 = jnp.float32
    bsz, s, _ = q.shape
    nc = s // CHUNK
    prev = ATT_BAND - 1
    band = ATT_BAND * CHUNK
    qh = q.reshape(bsz, s, ATT_HEADS, ATT_HD)
    pad = ((0, 0), (prev * CHUNK, 0), (0, 0), (0, 0))
    kp = jnp.pad(k.reshape(bsz, s, ATT_HEADS, ATT_HD), pad)
    vp = jnp.pad(v.reshape(bsz, s, ATT_HEADS, ATT_HD), pad)
    l_idx = jnp.arange(CHUNK)
    m_idx = jnp.arange(band)
    rel = l_idx[:, None] + prev * CHUNK - m_idx[None, :]
    idx = jnp.clip(rel, -(CHUNK - 1), MAX_REL) + (CHUNK - 1)
    bias = rel_bias.astype(f32)[:, idx]
    scale = ATT_HD ** -0.5

    def one_chunk(c):
        qc = lax.dynamic_slice_in_dim(qh, c * CHUNK, CHUNK, axis=1)
        kc = lax.dynamic_slice_in_dim(kp, c * CHUNK, band, axis=1)
        vc = lax.dynamic_slice_in_dim(vp, c * CHUNK, band, axis=1)
        sc = jnp.einsum('blhd,bmhd->bhlm', qc, kc, preferred_element_type=f32) * scale + bias[None]
        key_ok = m_idx >= (prev - c) * CHUNK
        sc = jnp.where(key_ok[None, None, None, :], sc, NEG_INF)
        pw = jax.nn.softmax(sc, axis=-1)
        return jnp.einsum('bhlm,bmhd->blhd', pw.astype(vc.dtype), vc)

    out = lax.map(one_chunk, jnp.arange(nc))
    return out.transpose(1, 0, 2, 3, 4).reshape(bsz, s, ATT_HEADS * ATT_HD)


def conformer_conv(a, g, dw_w, dw_b, ln_g, ln_b):
    y = a * jax.nn.sigmoid(g)
    y = lax.conv_general_dilated(y, dw_w[:, None, :], window_strides=(1,),
                                 padding=((CONV_K - 1, 0),),
                                 dimension_numbers=('NWC', 'WIO', 'NWC'),
                                 feature_group_count=CONV_WIDTH) + dw_b
    return jax.nn.silu(layer_norm(y, ln_g, ln_b))


def setup_inputs(seed: int = 0) -> dict:
    key = jax.random.key(seed)
    ks = iter(jax.random.split(key, 40))

    def nrm(shape, scale):
        return scale * jax.random.normal(next(ks), shape, jnp.float32)

    def gain(shape):
        return 1.0 + nrm(shape, 0.05)

    L = DEPTH
    return {
        "x": nrm((BATCH, SEQ, D_MODEL), 1.0),
        "p": nrm((DEPTH, BATCH, SEQ, PLE_DIM), 1.0),
        "norm1_g": gain((L, D_MODEL)),
        "w_in": nrm((L, D_MODEL, IN_COLS), D_MODEL ** -0.5),
        "sg_ln_g": gain((L, SG_WIDTH)),
        "sg_ln_b": nrm((L, SG_WIDTH), 0.02),
        "sg_w": nrm((L, SG_GROUPS, SG_BLOCK, SG_BLOCK), SG_BLOCK ** -0.5),
        "sg_b": 1.0 + nrm((L, SG_GROUPS, SG_BLOCK), 0.1),
        "gla_w_a2": nrm((L, GLA_RANK, GLA_HEADS * GLA_DK), GLA_RANK ** -0.5),
        "gla_b_a": nrm((L, GLA_HEADS * GLA_DK), 0.1),
        "gla_norm_g": gain((L, GLA_HEADS * GLA_DV)),
        "att_rel_bias": nrm((L, ATT_HEADS, REL_TABLE), 0.5),
        "conv_dw_w": nrm((L, CONV_K, CONV_WIDTH), CONV_K ** -0.5),
        "conv_dw_b": nrm((L, CONV_WIDTH), 0.02),
        "conv_ln_g": gain((L, CONV_WIDTH)),
        "conv_ln_b": nrm((L, CONV_WIDTH), 0.02),
        "w_branch": nrm((L, N_BRANCH, BRANCH_WIDTH, D_MODEL), BRANCH_WIDTH ** -0.5),
        "w_gate": nrm((L, N_BRANCH, D_MODEL, D_MODEL), D_MODEL ** -0.5),
        "b_gate": nrm((L, N_BRANCH, D_MODEL), 0.02),
        "w_out": nrm((L, D_MODEL, D_MODEL), D_MODEL ** -0.5),
        "norm2_g": gain((L, D_MODEL)),
        "w_ff1": nrm((L, D_MODEL, D_FF), D_MODEL ** -0.5),
        "w_ff2": nrm((L, D_FF, D_MODEL), D_FF ** -0.5),
        "norm3_g": gain((L, D_MODEL)),
        "w_ple_gate": nrm((L, D_MODEL, D_MODEL), D_MODEL ** -0.5),
        "b_ple_gate": nrm((L, D_MODEL), 0.02),
        "w_ple": nrm((L, PLE_DIM, D_MODEL), PLE_DIM ** -0.5),
        "final_g": gain((D_MODEL,)),
    }


def reference(x, p, norm1_g, w_in, sg_ln_g, sg_ln_b, sg_w, sg_b, gla_w_a2, gla_b_a, gla_norm_g,
              att_rel_bias, conv_dw_w, conv_dw_b, conv_ln_g, conv_ln_b, w_branch, w_gate, b_gate,
              w_out, norm2_g, w_ff1, w_ff2, norm3_g, w_ple_gate, b_ple_gate, w_ple, final_g):
    cuts = np.cumsum(IN_SPLITS)[:-1].tolist()
    h = x
    for i in range(DEPTH):
        xn = rms_norm(h, norm1_g[i])
        proj = jnp.einsum('bsd,dk->bsk', xn, w_in[i])
        (sg_u, sg_v, g_q, g_k, g_v, g_r, g_a, a_q, a_k, a_v, c_a, c_g) = jnp.split(proj, cuts, axis=-1)

        y_a = spatial_gating(jax.nn.gelu(sg_u), jax.nn.gelu(sg_v), sg_ln_g[i], sg_ln_b[i], sg_w[i], sg_b[i])
        y_b = gated_linear_attention(g_q, g_k, g_v, g_r, g_a, gla_w_a2[i], gla_b_a[i], gla_norm_g[i])
        y_c = band_chunk_attention(a_q, a_k, a_v, att_rel_bias[i])
        y_d = conformer_conv(c_a, c_g, conv_dw_w[i], conv_dw_b[i], conv_ln_g[i], conv_ln_b[i])

        merged = jnp.zeros_like(h)
        for n, y in enumerate((y_a, y_b, y_c, y_d)):
            gate = jax.nn.sigmoid(jnp.einsum('bsd,de->bse', xn, w_gate[i, n]) + b_gate[i, n])
            merged = merged + gate * jnp.einsum('bsk,kd->bsd', y, w_branch[i, n])
        h = h + jnp.einsum('bsd,de->bse', merged, w_out[i])

        hn = rms_norm(h, norm2_g[i])
        ff = jnp.square(jax.nn.relu(jnp.einsum('bsd,df->bsf', hn, w_ff1[i])))
        h = h + jnp.einsum('bsf,fd->bsd', ff, w_ff2[i])

        hg = rms_norm(h, norm3_g[i])
        ple_gate = jax.nn.sigmoid(jnp.einsum('bsd,de->bse', hg, w_ple_gate[i]) + b_ple_gate[i])
        h = h + ple_gate * jnp.einsum('bsq,qd->bsd', p[i], w_ple[i])
    return rms_norm(h, final_g)
```

```python
import numpy as np
from contextlib import ExitStack
import concourse.bass as bass
import concourse.mybir as mybir
from concourse.bass_utils import run_bass_kernel_spmd

F32 = mybir.dt.float32
BF16 = mybir.dt.bfloat16
AF = mybir.ActivationFunctionType
ALU = mybir.AluOpType
AX = mybir.AxisListType

T = 512
D = 1024
EPS = 1e-6
NEGB = -30.0
HORD = [0, 2, 1, 3]

PP_N1, PP_N2, PP_N3 = 0, 8, 16
PP_GNG, PP_CB, PP_CLG, PP_CLB = 24, 28, 32, 36
PP_BG = 40
PP_CW = 72
NPP = 196


class Sched:
    CH = 30000

    def __init__(self, nc):
        self.nc = nc
        self.engs = {"pe": nc.tensor, "act": nc.scalar, "dve": nc.vector,
                     "pool": nc.gpsimd, "sp": nc.sync}
        self.rec = {e: [] for e in self.engs}
        self.cnt = {}
        self.last_w = {}
        self.readers = {}
        self.known = {e: {} for e in self.engs}
        self.clock = {}
        self.nwaits = 0
        self.nops = 0

    def _deps(self, reads, writes, me_stream=None):
        deps = {}
        for r in reads:
            d = self.last_w.get(r)
            if d is not None and deps.get(d[0], -1) < d[1]:
                deps[d[0]] = d[1]
            if r.startswith("pb"):
                rd = self.readers.get(r)
                if rd:
                    for s, i in rd.items():
                        if s != me_stream and deps.get(s, -1) < i:
                            deps[s] = i
        for w in writes:
            d = self.last_w.get(w)
            if d is not None and deps.get(d[0], -1) < d[1]:
                deps[d[0]] = d[1]
            rd = self.readers.get(w)
            if rd:
                for s, i in rd.items():
                    if deps.get(s, -1) < i:
                        deps[s] = i
        return deps

    def _mark(self, me, reads, writes):
        s, i = me
        for r in reads:
            rd = self.readers.get(r)
            if rd is None:
                self.readers[r] = {s: i}
            else:
                rd[s] = i
        for w in writes:
            self.last_w[w] = me
            self.readers[w] = {}

    def _waits(self, eng, deps, stream_self):
        kn = self.known[eng]
        waits = []
        for s, i in deps.items():
            if s == stream_self:
                if eng == "pe":
                    continue
            if kn.get(s, -1) >= i:
                continue
            waits.append((s, i))
            ck = self.clock.get((s, i))
            if ck:
                for s2, i2 in ck.items():
                    if kn.get(s2, -1) < i2:
                        kn[s2] = i2
            kn[s] = i
        self.nwaits += len(waits)
        return waits

    def op(self, eng, fn, reads=(), writes=()):
        deps = self._deps(reads, writes, eng)
        waits = self._waits(eng, deps, eng)
        idx = self.cnt.get(eng, 0)
        self.cnt[eng] = idx + 1
        self.clock[(eng, idx)] = dict(self.known[eng])
        self.rec[eng].append((fn, waits, (eng, idx), 1))
        self._mark((eng, idx), reads, writes)
        self.nops += 1

    def dma(self, qeng, slot, out, in_, reads=(), writes=()):
        if slot is None:
            self.nuniq = getattr(self, "nuniq", 0) + 1
            slot = f"u{self.nuniq}"
        stream = "d:" + slot
        deps = self._deps(reads, writes)
        waits = self._waits(qeng, deps, None)
        idx = self.cnt.get(stream, 0)
        self.cnt[stream] = idx + 1
        self.clock[(stream, idx)] = dict(self.known[qeng])

        def fn(e, out=out, in_=in_):
            return e.dma_start(out=out, in_=in_)
        self.rec[qeng].append((fn, waits, (stream, idx), 16))
        self._mark((stream, idx), reads, writes)
        self.nops += 1

    def finish_wait(self, eng, resources):
        deps = self._deps(resources, ())
        waits = self._waits(eng, deps, eng)
        self.rec[eng].append((None, waits, None, 0))

    def _semval(self, stream, idx, sems):
        if stream.startswith("d:"):
            per = self.CH // 16
            return sems[(stream, idx // per)], 16 * (idx % per + 1)
        return sems[(stream, idx // self.CH)], idx % self.CH + 1

    def emit(self):
        nc = self.nc
        need = []
        for s, c in self.cnt.items():
            per = self.CH // 16 if s.startswith("d:") else self.CH
            for ep in range((c + per - 1) // per):
                need.append((s, ep))
        print(f"[sched] ops={self.nops} waits={self.nwaits} sems={len(need)} "
              f"cnt={ {k: v for k, v in self.cnt.items() if not k.startswith('d:')} }", flush=True)
        with ExitStack() as st:
            sems = {}
            for k in need:
                sems[k] = st.enter_context(nc.semaphore(f"s_{k[0].replace(':', '_')}_{k[1]}"))
            block = st.enter_context(nc.Block())

            def run(engname):
                def body(e):
                    for fn, waits, me, inc in self.rec[engname]:
                        for (s, i) in waits:
                            sem, val = self._semval(s, i, sems)
                            e.wait_ge(sem, val)
                        if fn is None:
                            continue
                        inst = fn(e)
                        sem, _ = self._semval(me[0], me[1], sems)
                        inst.then_inc(sem, inc)
                return body
            block.tensor(run("pe"))
            block.scalar(run("act"))
            block.vector(run("dve"))
            block.gpsimd(run("pool"))
            block.sync(run("sp"))


class Prog:
    def __init__(self, kind, NT):
        self.kind = kind
        self.NT = NT
        self.nc = bass.Bass("TRN2", target_bir_lowering=False)
        self.S = Sched(self.nc)
        self.st = ExitStack()
        self.uid = 0
        self.wslot = 0
        self.fl_i = 0
        self.bl_i = 0
        self.rot_i = 0

    def din(self, name, shape, dt=F32):
        return self.nc.dram_tensor(name, list(shape), dt, kind="ExternalInput").ap()

    def dout(self, name, shape, dt=F32):
        return self.nc.dram_tensor(name, list(shape), dt, kind="ExternalOutput").ap()

    def sb(self, name, shape, dt):
        return self.st.enter_context(self.nc.sbuf_tensor("s_" + name, list(shape), dt))

    def ps(self, name, shape, dt):
        return self.st.enter_context(self.nc.psum_tensor(name, list(shape), dt))

    def setup_common(self):
        nc = self.nc
        self.banks = [(self.ps(f"pb{i}", [128, 512], F32), f"pb{i}") for i in range(6)]
        self.rot = list(range(6))
        self.pbts = [self.ps(f"pbt{i}", [128, 1024], BF16) for i in range(2)]
        self.pbt_i = 0
        NF, NB = 8, 8
        self.ftiles = [(self.sb(f"ft{i}", [128, 512], F32), f"ft{i}") for i in range(NF)]
        self.btiles = [(self.sb(f"bt{i}", [128, 512], BF16), f"bt{i}") for i in range(NB)]
        self.wbuf = [self.sb(f"wb{i}", [128, 8, 528], BF16) for i in range(3)]
        self.h = self.sb("h", [128, 4, D], F32)
        self.xn = self.sb("xn", [128, 4, D], BF16)
        self.xnT = self.sb("xnT", [128, 8, T], BF16)
        self.small = self.sb("small", [128, 64], F32)
        self.small_i = 0
        self.pp = self.sb("pp", [128, NPP], F32)
        self.ident_b = self.sb("ident_b", [128, 128], BF16)
        self.ident_f = self.sb("ident_f", [128, 128], F32)
        self.T2 = self.sb("T2", [128, 128], F32)
        self.cind = self.sb("cind", [128, 2], F32)
        self.ones_b = self.sb("ones_b", [128, 128], BF16)
        self.ones_f = self.sb("ones_f", [128, 128], F32)
        self.wa2 = self.sb("wa2", [16, 256], BF16)
        self.ba_bc = self.sb("ba_bc", [128, 256], F32)
        self.Sst = self.sb("Sst", [128, 2, 128], F32)
        self.dec = self.sb("dec", [128, 16], F32)
        S = self.S
        c_ident = self.din("c_ident", [128, 128])
        c_T2 = self.din("c_T2", [128, 128])
        c_cind = self.din("c_cind", [128, 2])
        S.dma("sp", None, self.ident_f[:], c_ident, writes=["ident_f"])
        S.dma("pool", None, self.ident_b[:], c_ident, writes=["ident_b"])
        S.dma("sp", None, self.T2[:], c_T2, writes=["T2"])
        S.dma("sp", None, self.cind[:], c_cind, writes=["cind"])
        S.op("dve", lambda e: e.memset(self.ones_f[:], 1.0), writes=["ones_f"])
        S.op("dve", lambda e: e.memset(self.ones_b[:], 1.0), writes=["ones_b"])

    def bank(self):
        i = self.rot[self.rot_i % len(self.rot)]
        self.rot_i += 1
        return self.banks[i]

    def ft(self):
        t = self.ftiles[self.fl_i % len(self.ftiles)]
        self.fl_i += 1
        return t

    def bt(self):
        t = self.btiles[self.bl_i % len(self.btiles)]
        self.bl_i += 1
        return t

    def sm(self, n=1):
        if self.small_i + 4 > 64:
            self.small_i = 0
        i = self.small_i
        self.small_i += 4
        return self.small[:, i:i + n], f"sm{i}"

    def hold(self, n):
        got = []
        for _ in range(n):
            i = self.rot.pop(self.rot_i % len(self.rot))
            got.append(i)
        return got

    def release(self, idxs):
        self.rot.extend(idxs)

    def load_w(self, view, k0, KC, c0, c1):
        slot = self.wslot % 3
        self.wslot += 1
        key = f"w{slot}"
        self.S.dma("pool", key, self.wbuf[slot][:, 0:KC, 0:c1 - c0], view[:, k0:k0 + KC, c0:c1], writes=[key])
        return self.wbuf[slot], key

    def declare_layer(self, sfx, full):
        d = {}
        d["w_in"] = self.din("w_in" + sfx, [D, 5136]).rearrange("(kc p) n -> p kc n", p=128)
        d["pp"] = self.din("pp" + sfx, [128, NPP])
        d["w_a2"] = self.din("w_a2" + sfx, [16, 256])
        d["b_a"] = self.din("b_a" + sfx, [256])
        if full:
            d["w_gate"] = [self.din(f"w_gate{n}" + sfx, [D, D]).rearrange("(kc p) n -> p kc n", p=128) for n in range(4)]
            d["w_branch"] = [self.din(f"w_branch{n}" + sfx, [512, D]).rearrange("(kc p) n -> p kc n", p=128) for n in range(4)]
            d["w_out"] = self.din("w_out" + sfx, [D, D]).rearrange("(kc p) n -> p kc n", p=128)
            d["w_ff1"] = self.din("w_ff1" + sfx, [D, 4096]).rearrange("(kc p) n -> p kc n", p=128)
            d["w_ff2"] = self.din("w_ff2" + sfx, [4096, D]).rearrange("(kc p) n -> p kc n", p=128)
            d["w_pg"] = self.din("w_pg" + sfx, [D, D]).rearrange("(kc p) n -> p kc n", p=128)
            d["w_ple"] = self.din("w_ple" + sfx, [256, D]).rearrange("(kc p) n -> p kc n", p=128)
            d["b_pg"] = self.din("b_pg" + sfx, [1, D])
            d["sg_lng"] = self.din("sg_lng" + sfx, [512])
            d["sg_lnb"] = self.din("sg_lnb" + sfx, [512])
            d["sg_wT"] = self.din("sg_wT" + sfx, [128, 4, 128])
            d["sg_b"] = self.din("sg_b" + sfx, [512])
            d["biasT"] = self.din("biasT" + sfx, [128, 5 * 8 * 128])
        return d

    def setup_full(self):
        self.uT = self.sb("uT", [128, 4, T], BF16)
        self.qT = self.sb("qT", [128, 2, T], BF16)
        self.rT = self.sb("rT", [128, 4, T], BF16)
        self.aqT = self.sb("aqT", [128, 4, T], BF16)
        self.aT = self.sb("aT", [16, T], BF16)
        self.vln = self.sb("vln", [128, 4, 512], BF16)
        self.vg = self.sb("vg", [128, 4, 512], BF16)
        self.kf = self.sb("kf", [128, 4, 256], F32)
        self.kdec = self.sb("kdec", [128, 4, 256], BF16)
        self.kring = self.sb("kring", [128, 4, 1024], BF16)
        self.vring = self.sb("vring", [128, 8, 8, 64], BF16)
        self.vone = self.sb("vone", [128, 8, 8], BF16)
        self.biasT = self.sb("biasT", [128, 5, 8, 128], BF16)
        self.ybuf = self.sb("ybuf", [128, 4, 544], F32)
        self.cv = self.sb("cv", [128, 4, T], F32)
        self.yT = self.sb("yT", [128, 16, T], BF16)
        self.mT = self.sb("mT", [128, 8, T], BF16)
        self.pT = self.sb("pT", [128, 2, T], BF16)
        self.ptile = self.sb("ptile", [128, 4, 256], F32)
        self.yc = self.sb("yc", [128, 2, 512], BF16)
        self.Sbf = self.sb("Sbf", [128, 2, 2, 128], BF16)
        self.lng_bc = self.sb("lng_bc", [128, 512], F32)
        self.lnb_bc = self.sb("lnb_bc", [128, 512], F32)
        self.bs_bc = self.sb("bs_bc", [128, 512], F32)
        self.wmT = self.sb("wmT", [128, 4, 128], BF16)
        self.bpg = self.sb("bpg", [1, D], BF16)
        self.fg_bc = self.sb("fg_bc", [128, D], F32)
        self.flag = self.sb("flag", [128, 1], F32)

    def setup_summary_bufs(self):
        self.aT = self.sb("aT", [16, T], BF16)
        self.vg = self.sb("vg", [128, 4, 512], BF16)
        self.kf = self.sb("kf", [128, 4, 256], F32)
        self.kdec = self.sb("kdec", [128, 4, 256], BF16)
        self.dlog = self.sb("dlog", [128, 2], F32)

    def load_layer_params(self, L, full):
        S = self.S
        S.dma("sp", None, self.pp[:], L["pp"], writes=["pp"])
        S.dma("pool", None, self.wa2[:], L["w_a2"], writes=["wa2"])
        S.dma("sp", None, self.ba_bc[:], L["b_a"].partition_broadcast(128), writes=["ba_bc"])
        if full:
            S.dma("sp", None, self.lng_bc[:], L["sg_lng"].partition_broadcast(128), writes=["lng_bc"])
            S.dma("sp", None, self.lnb_bc[:], L["sg_lnb"].partition_broadcast(128), writes=["lnb_bc"])
            S.dma("sp", None, self.bs_bc[:], L["sg_b"].partition_broadcast(128), writes=["bs_bc"])
            S.dma("pool", None, self.wmT[:], L["sg_wT"], writes=["wmT"])
            S.op("dve", lambda e: e.memset(self.wmT[64:128, :, 0:64], 0.0), reads=["wmT"], writes=["wmT"])
            S.dma("pool", None, self.bpg[:], L["b_pg"], writes=["bpg"])
            S.dma("pool", None, self.biasT[:], L["biasT"].rearrange("p (j h l) -> p j h l", j=5, h=8), writes=["biasT"])

    def norm_T(self, gcol):
        S = self.S
        h, xn, xnT, pp = self.h, self.xn, self.xnT, self.pp
        for s in range(4):
            ss, kss = self.sm()
            S.op("dve", lambda e, ss=ss: e.memset(ss, 0.0), writes=[kss])
            S.op("act", lambda e, s=s, ss=ss: e.activation(out=xn[:, s, :], in_=h[:, s, :], func=AF.Square, accum_out=ss),
                 reads=[f"h{s}"], writes=[f"xn{s}", kss])
            S.op("act", lambda e, ss=ss: e.activation(out=ss, in_=ss, func=AF.Ln, scale=1.0 / D, bias=EPS), reads=[kss], writes=[kss])
            S.op("act", lambda e, ss=ss: e.activation(out=ss, in_=ss, func=AF.Exp, scale=-0.5), reads=[kss], writes=[kss])
            S.op("dve", lambda e, s=s, ss=ss: e.tensor_scalar(out=xn[:, s, :], in0=h[:, s, :], scalar1=ss, scalar2=None, op0=ALU.mult),
                 reads=[f"h{s}", kss], writes=[f"xn{s}"])
        for c in range(8):
            half = self.pbt_i % 2
            self.pbt_i += 1
            pk = f"pbt{half}"
            pbt = self.pbts[half]
            for s in range(4):
                S.op("pe", lambda e, s=s, c=c, pbt=pbt: e.transpose(pbt[:, s * 128:(s + 1) * 128],
                                                                    xn[:, s, c * 128:(c + 1) * 128], self.ident_b[:]),
                     reads=[f"xn{s}", "ident_b"], writes=[pk])
            eng = "dve" if c % 2 == 0 else "act"
            if eng == "dve":
                S.op("dve", lambda e, c=c, pbt=pbt: e.tensor_scalar(out=xnT[:, c, :], in0=pbt[:, 0:512],
                                                                       scalar1=pp[:, gcol + c:gcol + c + 1], scalar2=None, op0=ALU.mult),
                     reads=[pk, "pp"], writes=[f"xnT{c}"])
            else:
                S.op("act", lambda e, c=c, pbt=pbt: e.activation(out=xnT[:, c, :], in_=pbt[:, 0:512],
                                                                    func=AF.Copy, scale=pp[:, gcol + c:gcol + c + 1]),
                     reads=[pk, "pp"], writes=[f"xnT{c}"])

    def proj_feat(self, w, wk, KC, ncols, rhs_fn, rhs_keys, evac, col0=0):
        S = self.S
        for m in range(ncols // 128 if ncols >= 128 else 1):
            M = min(128, ncols)
            b, bk = self.bank()
            for kc in range(KC):
                S.op("pe", lambda e, b=b, kc=kc, m=m, M=M: e.matmul(b[0:M, :], w[:, kc, col0 + m * 128: col0 + m * 128 + M], rhs_fn(kc),
                                                                  start=(kc == 0), stop=(kc == KC - 1)),
                     reads=[wk, rhs_keys(kc)], writes=[bk])
            evac(m, b, bk)

    def proj_tok(self, w, wk, KC, ncols, lhs_fn, lhs_keys, evac, col0=0):
        S = self.S
        for s in range(4):
            b, bk = self.bank()
            for kc in range(KC):
                S.op("pe", lambda e, b=b, kc=kc, s=s: e.matmul(b[:, 0:ncols], lhs_fn(kc, s), w[:, kc, col0:col0 + ncols],
                                                               start=(kc == 0), stop=(kc == KC - 1)),
                     reads=[wk, lhs_keys(kc)], writes=[bk])
            evac(s, b, bk)

    def xnT_rhs(self, kc):
        return self.xnT[:, kc, :]

    def xnT_key(self, kc):
        return f"xnT{kc}"

    def xnT_lhs(self, kc, s):
        return self.xnT[:, kc, s * 128:(s + 1) * 128]

    def gla_tile(self, do_read):
        S = self.S
        kf, kdec, vg, aT = self.kf, self.kdec, self.vg, self.aT
        bd_i = self.hold(1)
        bd, bdk = self.banks[bd_i[0]]
        for s in range(4):
            bz, bzk = self.bank()
            S.op("pe", lambda e, bz=bz, s=s: e.matmul(bz[:, 0:256], aT[0:16, s * 128:(s + 1) * 128], self.wa2[0:16, :], start=True, stop=True),
                 reads=["aT", "wa2"], writes=[bzk])
            zt, ztk = self.ft()
            S.op("dve", lambda e, bz=bz, zt=zt: e.tensor_tensor(out=zt[:, 0:256], in0=bz[:, 0:256], in1=self.ba_bc[:], op=ALU.add),
                 reads=[bzk, "ba_bc"], writes=[ztk])
            S.op("act", lambda e, zt=zt: e.activation(out=zt[:, 0:256], in_=zt[:, 0:256], func=AF.Exp, scale=-1.0), reads=[ztk], writes=[ztk])
            S.op("act", lambda e, zt=zt: e.activation(out=zt[:, 0:256], in_=zt[:, 0:256], func=AF.Ln, bias=1.0), reads=[ztk], writes=[ztk])
            be, bek = self.bank()
            S.op("pe", lambda e, be=be, zt=zt: e.matmul(be[:, 0:256], self.T2[:], zt[:, 0:256], start=True, stop=True),
                 reads=["T2", ztk], writes=[bek])
            S.op("act", lambda e, be=be, zt=zt: e.activation(out=zt[:, 256:512], in_=be[:, 0:256], func=AF.Exp), reads=[bek, ztk], writes=[ztk])
            S.op("dve", lambda e, zt=zt, s=s: e.tensor_tensor(out=kdec[:, s, :], in0=kf[:, s, :], in1=zt[:, 256:512], op=ALU.mult),
                 reads=[f"kf{s}", ztk], writes=[f"kdec{s}"])
            for hp in range(2):
                S.op("pe", lambda e, zt=zt, s=s, hp=hp: e.matmul(bd[:, hp * 8 + s * 2: hp * 8 + s * 2 + 2], zt[:, hp * 128:(hp + 1) * 128], self.cind[:],
                                                                 start=True, stop=True),
                     reads=[ztk, "cind"], writes=[bdk])
        S.op("act", lambda e: e.activation(out=self.dec[:], in_=bd[:, 0:16], func=AF.Exp), reads=[bdk], writes=["dec"])
        if not do_read:
            dl, dlk = self.sm(2)
            S.op("dve", lambda e, dl=dl: e.tensor_reduce(out=dl, in_=bd[:, 0:16].rearrange("p (a c) -> p a c", a=2), axis=AX.X, op=ALU.add),
                 reads=[bdk], writes=[dlk])
            S.op("dve", lambda e, dl=dl: e.tensor_tensor(out=self.dlog[:], in0=self.dlog[:], in1=dl, op=ALU.add), reads=[dlk, "dlog"], writes=["dlog"])
        self.release(bd_i)
        bo = None
        if do_read:
            held = self.hold(4)
            bo = [self.banks[i] for i in held]
        Sst = self.Sst
        for c in range(8):
            s, half = c // 2, c % 2
            r0 = half * 64
            for hp in range(2):
                bu, buk = self.bank()
                S.op("pe", lambda e, bu=bu, s=s, r0=r0, hp=hp: e.matmul(bu[:, 0:256], kdec[r0:r0 + 64, s, hp * 128:(hp + 1) * 128],
                                                                        vg[r0:r0 + 64, s, hp * 256:(hp + 1) * 256], start=True, stop=True),
                     reads=[f"kdec{s}", f"vg{s}"], writes=[buk])
                for hh in range(2):
                    p0 = hh * 64
                    S.op("dve", lambda e, bu=bu, hp=hp, p0=p0, hh=hh, c=c: e.scalar_tensor_tensor(
                        out=Sst[p0:p0 + 64, hp, :], in0=Sst[p0:p0 + 64, hp, :], scalar=self.dec[p0:p0 + 64, hp * 8 + c: hp * 8 + c + 1],
                        in1=bu[p0:p0 + 64, hh * 128:(hh + 1) * 128], op0=ALU.mult, op1=ALU.add),
                        reads=[buk, "dec", f"Sst{hp}{hh}"], writes=[f"Sst{hp}{hh}"])
            if do_read:
                par = c % 2
                S.op("act", lambda e, par=par: e.copy(out=self.Sbf[:, par, :, :], in_=Sst[:, :, :]),
                     reads=["Sst00", "Sst01", "Sst10", "Sst11"], writes=[f"Sbf{par}"])
                for hd in range(4):
                    hp, r1 = hd // 2, (hd % 2) * 64
                    ob, obk = bo[hd]
                    S.op("pe", lambda e, ob=ob, hp=hp, r1=r1, par=par, c=c: e.matmul(ob[:, c * 64:(c + 1) * 64], self.Sbf[r1:r1 + 64, par, hp, :],
                                                                                 self.qT[r1:r1 + 64, hp, c * 64:(c + 1) * 64], start=True, stop=True),
                         reads=[f"Sbf{par}", f"qT{hp}"], writes=[obk])
        if do_read:
            for hd in range(4):
                ob, obk = bo[hd]
                sq, sqk = self.bt()
                S.op("act", lambda e, ob=ob, sq=sq: e.activation(out=sq[:], in_=ob[:], func=AF.Square), reads=[obk], writes=[sqk])
                bs, bsk = self.bank()
                S.op("pe", lambda e, bs=bs, sq=sq: e.matmul(bs[:], self.ones_b[:], sq[:], start=True, stop=True), reads=["ones_b", sqk], writes=[bsk])
                rs, rsk = self.ft()
                S.op("act", lambda e, bs=bs, rs=rs: e.activation(out=rs[:], in_=bs[:], func=AF.Ln, scale=1.0 / 128, bias=EPS), reads=[bsk], writes=[rsk])
                S.op("act", lambda e, rs=rs: e.activation(out=rs[:], in_=rs[:], func=AF.Exp, scale=-0.5), reads=[rsk], writes=[rsk])
                S.op("dve", lambda e, ob=ob, rs=rs, hd=hd: e.scalar_tensor_tensor(out=rs[:], in0=ob[:], scalar=self.pp[:, PP_GNG + hd:PP_GNG + hd + 1],
                                                                                 in1=rs[:], op0=ALU.mult, op1=ALU.mult),
                     reads=[obk, rsk, "pp"], writes=[rsk])
                S.op("dve", lambda e, rs=rs, hd=hd: e.tensor_tensor(out=self.yT[:, 4 + hd, :], in0=rs[:], in1=self.rT[:, hd, :], op=ALU.mult),
                     reads=[rsk, f"rT{hd}"], writes=[f"yT{4 + hd}"])
            self.release(held)

    def mixer_tile(self, L, t, halo):
        S = self.S
        W = L["w_in"]
        xr, xk, xl = self.xnT_rhs, self.xnT_key, self.xnT_lhs
        kcol = 0 if halo else 512 * ((t + 1) % 2)
        slot0 = 0 if halo else 4 * ((t + 1) % 2)

        if not halo:
            w, wk = self.load_w(W, 0, 8, 0, 512)
            self.proj_feat(w, wk, 8, 512, xr, xk, lambda m, b, bk: S.op(
                "act", lambda e: e.activation(out=self.uT[:, m, :], in_=b[:], func=AF.Gelu), reads=[bk], writes=[f"uT{m}"]))
            w, wk = self.load_w(W, 0, 8, 512, 1024)

            def ev_v(s, b, bk):
                vt, vtk = self.ft()
                s1, s1k = self.sm()
                s2, s2k = self.sm()
                mn, mnk = self.sm()
                S.op("act", lambda e: e.activation(out=vt[:], in_=b[:], func=AF.Gelu), reads=[bk], writes=[vtk])
                S.op("dve", lambda e: e.tensor_reduce(out=s1, in_=vt[:], axis=AX.X, op=ALU.add), reads=[vtk], writes=[s1k])
                jk, jkk = self.bt()
                S.op("dve", lambda e: e.memset(s2, 0.0), writes=[s2k])
                S.op("act", lambda e: e.activation(out=jk[:], in_=vt[:], func=AF.Square, accum_out=s2), reads=[vtk], writes=[jkk, s2k])
                S.op("dve", lambda e: e.tensor_scalar(out=mn, in0=s1, scalar1=1.0 / 512, scalar2=None, op0=ALU.mult), reads=[s1k], writes=[mnk])
                S.op("dve", lambda e: e.scalar_tensor_tensor(out=s1, in0=mn, scalar=-1.0, in1=mn, op0=ALU.mult, op1=ALU.mult), reads=[mnk, s1k], writes=[s1k])
                S.op("dve", lambda e: e.scalar_tensor_tensor(out=s2, in0=s2, scalar=1.0 / 512, in1=s1, op0=ALU.mult, op1=ALU.add), reads=[s1k, s2k], writes=[s2k])
                S.op("act", lambda e: e.activation(out=s2, in_=s2, func=AF.Ln, bias=EPS), reads=[s2k], writes=[s2k])
                S.op("act", lambda e: e.activation(out=s2, in_=s2, func=AF.Exp, scale=-0.5), reads=[s2k], writes=[s2k])
                S.op("dve", lambda e: e.tensor_scalar(out=vt[:], in0=vt[:], scalar1=mn, scalar2=s2, op0=ALU.subtract, op1=ALU.mult),
                     reads=[vtk, mnk, s2k], writes=[vtk])
                S.op("dve", lambda e: e.tensor_tensor(out=vt[:], in0=vt[:], in1=self.lng_bc[:], op=ALU.mult), reads=[vtk, "lng_bc"], writes=[vtk])
                S.op("dve", lambda e: e.tensor_tensor(out=self.vln[:, s, :], in0=vt[:], in1=self.lnb_bc[:], op=ALU.add), reads=[vtk, "lnb_bc"], writes=[f"vln{s}"])
            self.proj_tok(w, wk, 8, 512, xl, xk, ev_v)
            for g in range(4):
                bm, bmk = self.bank()
                for s in range(4):
                    S.op("pe", lambda e, bm=bm, s=s, g=g: e.matmul(bm[:, s * 128:(s + 1) * 128], self.vln[:, s, g * 128:(g + 1) * 128], self.wmT[:, g, :],
                                                                     start=True, stop=True), reads=[f"vln{s}", "wmT"], writes=[bmk])
                tm, tmk = self.ft()
                for s in range(4):
                    S.op("dve", lambda e, bm=bm, tm=tm, s=s, g=g: e.tensor_tensor(out=tm[:, s * 128:(s + 1) * 128], in0=bm[:, s * 128:(s + 1) * 128],
                                                                                    in1=self.bs_bc[:, g * 128:(g + 1) * 128], op=ALU.add),
                         reads=[bmk, "bs_bc"], writes=[tmk])
                S.op("dve", lambda e, tm=tm, g=g: e.tensor_tensor(out=self.yT[:, g, :], in0=tm[:], in1=self.uT[:, g, :], op=ALU.mult),
                     reads=[tmk, f"uT{g}"], writes=[f"yT{g}"])
            w, wk = self.load_w(W, 0, 8, 1024, 1536)
            self.proj_feat(w, wk, 8, 256, xr, xk, lambda m, b, bk: S.op(
                "act", lambda e: e.activation(out=self.qT[:, m, :], in_=b[:], func=AF.Copy, scale=0.125), reads=[bk], writes=[f"qT{m}"]))
            self.proj_tok(w, wk, 8, 256, xl, xk, lambda s, b, bk: S.op(
                "dve", lambda e: e.tensor_copy(out=self.kf[:, s, :], in_=b[:, 0:256]), reads=[bk], writes=[f"kf{s}"]), col0=256)
            w, wk = self.load_w(W, 0, 8, 1536, 2048)
            self.proj_tok(w, wk, 8, 512, xl, xk, lambda s, b, bk: S.op(
                "act", lambda e: e.copy(out=self.vg[:, s, :], in_=b[:]), reads=[bk], writes=[f"vg{s}"]))
            w, wk = self.load_w(W, 0, 8, 2048, 2560)
            self.proj_feat(w, wk, 8, 512, xr, xk, lambda m, b, bk: S.op(
                "act", lambda e: e.activation(out=self.rT[:, m, :], in_=b[:], func=AF.Silu), reads=[bk], writes=[f"rT{m}"]))
            w, wk = self.load_w(W, 0, 8, 2560, 3088)
            self.proj_feat(w, wk, 8, 16, xr, xk, lambda m, b, bk: S.op(
                "dve", lambda e: e.tensor_copy(out=self.aT[0:16, :], in_=b[0:16, :]), reads=[bk], writes=["aT"]))
            self.proj_feat(w, wk, 8, 512, xr, xk, lambda m, b, bk: S.op(
                "act", lambda e: e.activation(out=self.aqT[:, m, :], in_=b[:], func=AF.Copy, scale=0.125), reads=[bk], writes=[f"aqT{m}"]), col0=16)
            import os
            KS = int(os.environ.get("KSTOP", "99"))
            if KS == 11:
                return
            self.gla_tile(True)
            if KS == 12:
                return

        w, wk = self.load_w(W, 0, 8, 3088, 3600)
        self.proj_feat(w, wk, 8, 512, xr, xk, lambda m, b, bk: S.op(
            "dve", lambda e: e.tensor_copy(out=self.kring[:, m, kcol:kcol + 512], in_=b[:]), reads=[bk], writes=[f"kr{m}_{kcol // 512}"]))
        w, wk = self.load_w(W, 0, 8, 3600, 4112)

        def ev_av(s, b, bk):
            sl = slot0 + s
            S.op("act", lambda e: e.copy(out=self.vring[:, sl, :, :], in_=b[:].rearrange("p (h e) -> p h e", h=8)), reads=[bk], writes=[f"vr{sl}"])
            if halo:
                S.op("dve", lambda e: e.tensor_scalar(out=self.vone[:, sl, :], in0=self.ones_f[:, 0:8], scalar1=self.flag[:, 0:1], scalar2=None, op0=ALU.mult),
                     reads=["ones_f", "flag"], writes=[f"vro{sl}"])
            else:
                S.op("dve", lambda e: e.memset(self.vone[:, sl, :], 1.0), writes=[f"vro{sl}"])
        self.proj_tok(w, wk, 8, 512, xl, xk, ev_av)

        wa, wak = self.load_w(W, 0, 8, 4112, 4624)
        wg, wgk = self.load_w(W, 0, 8, 4624, 5136)
        for m in range(4):
            ba, bak = self.bank()
            bg, bgk = self.bank()
            for kc in range(8):
                S.op("pe", lambda e, ba=ba, kc=kc, m=m: e.matmul(ba[:], wa[:, kc, m * 128:(m + 1) * 128], self.xnT[:, kc, :], start=(kc == 0), stop=(kc == 7)),
                     reads=[wak, f"xnT{kc}"], writes=[bak])
            for kc in range(8):
                S.op("pe", lambda e, bg=bg, kc=kc, m=m: e.matmul(bg[:], wg[:, kc, m * 128:(m + 1) * 128], self.xnT[:, kc, :], start=(kc == 0), stop=(kc == 7)),
                     reads=[wgk, f"xnT{kc}"], writes=[bgk])
            sg, sgk = self.ft()
            S.op("act", lambda e, bg=bg, sg=sg: e.activation(out=sg[:], in_=bg[:], func=AF.Sigmoid), reads=[bgk], writes=[sgk])
            S.op("dve", lambda e, ba=ba, sg=sg, m=m: e.tensor_tensor(out=self.ybuf[:, m, 30:542], in0=ba[:], in1=sg[:], op=ALU.mult),
                 reads=[bak, sgk], writes=[f"yb{m}"])
        if halo:
            for m in range(4):
                S.op("act", lambda e, m=m: e.copy(out=self.ybuf[:, m, 0:30], in_=self.ybuf[:, m, 512:542]), reads=[f"yb{m}"], writes=[f"ybh{m}"])
            return

        if KS == 13:
            return
        for s in range(4):
            for half in range(2):
                bpv, bpvk = self.ft()
                for j in range(5):
                    slot = (4 * t + s + j) % 8
                    bsA, bsAk = self.bank()
                    bsB, bsBk = self.bank()
                    for pos, hh in enumerate(HORD):
                        hd = 4 * half + hh
                        hp, r0 = hd // 2, (hd % 2) * 64
                        bsx, bsxk = (bsA, bsAk) if hh % 2 == 0 else (bsB, bsBk)
                        col = (pos % 2) * 128
                        S.op("pe", lambda e, bsx=bsx, col=col, hp=hp, r0=r0, slot=slot, s=s: e.matmul(
                            bsx[:, col:col + 128], self.kring[r0:r0 + 64, hp, slot * 128:(slot + 1) * 128],
                            self.aqT[r0:r0 + 64, hp, s * 128:(s + 1) * 128], start=True, stop=True),
                            reads=[f"kr{hp}_{slot // 4}", f"aqT{hp}"], writes=[bsxk])
                    tm, tmk = self.ft()
                    S.op("act", lambda e, bsA=bsA, tm=tm: e.copy(out=tm[:, 0:256], in_=bsA[:, 0:256]), reads=[bsAk], writes=[tmk])
                    S.op("act", lambda e, bsB=bsB, tm=tm: e.copy(out=tm[:, 256:512], in_=bsB[:, 0:256]), reads=[bsBk], writes=[tmk])
                    S.op("dve", lambda e, tm=tm, j=j, half=half: e.tensor_tensor(
                        out=tm[:], in0=tm[:], in1=self.biasT[:, j, 4 * half:4 * half + 4, :].rearrange("p h l -> p (h l)"), op=ALU.add),
                        reads=[tmk, "biasT"], writes=[tmk])
                    pt, ptk = self.bt()
                    S.op("act", lambda e, tm=tm, pt=pt: e.activation(out=pt[:], in_=tm[:], func=AF.Exp), reads=[tmk], writes=[ptk])
                    if KS == 21:
                        continue
                    bq, bqk = self.bank()
                    for pos, hh in enumerate(HORD):
                        hd = 4 * half + hh
                        S.op("pe", lambda e, bq=bq, hh=hh, hd=hd, pt=pt, slot=slot, pos=pos: e.matmul(
                            bq[:, hh * 64:(hh + 1) * 64], pt[:, pos * 128:(pos + 1) * 128], self.vring[:, slot, hd, :],
                            start=True, stop=True), reads=[ptk, f"vr{slot}"], writes=[bqk])
                        S.op("pe", lambda e, bq=bq, hh=hh, pt=pt, slot=slot, pos=pos: e.matmul(
                            bq[:, 256 + hh * 8:256 + hh * 8 + 8], pt[:, pos * 128:(pos + 1) * 128], self.vone[:, slot, :],
                            start=True, stop=True), reads=[ptk, f"vro{slot}"], writes=[bqk])
                    if j == 0:
                        S.op("dve", lambda e, bq=bq, bpv=bpv: e.tensor_copy(out=bpv[:, 0:288], in_=bq[:, 0:288]), reads=[bqk], writes=[bpvk])
                    else:
                        S.op("dve", lambda e, bq=bq, bpv=bpv: e.tensor_tensor(out=bpv[:, 0:288], in0=bpv[:, 0:288], in1=bq[:, 0:288], op=ALU.add),
                             reads=[bqk, bpvk], writes=[bpvk])
                if KS == 21:
                    continue
                for hh in range(4):
                    hd = 4 * half + hh
                    if hh == 0:
                        S.op("dve", lambda e, bpv=bpv: e.tensor_scalar(out=bpv[:, 256:288], in0=bpv[:, 256:288], scalar1=1e-30, scalar2=None, op0=ALU.max), reads=[bpvk], writes=[bpvk])
                        S.op("dve", lambda e, bpv=bpv: e.reciprocal(out=bpv[:, 288:320], in_=bpv[:, 256:288]), reads=[bpvk], writes=[bpvk])
                    S.op("dve", lambda e, bpv=bpv, hh=hh, hd=hd, s=s: e.tensor_scalar(
                        out=self.yc[:, s % 2, hd * 64:(hd + 1) * 64], in0=bpv[:, hh * 64:hh * 64 + 64], scalar1=bpv[:, 288 + 8 * hh:289 + 8 * hh], scalar2=None, op0=ALU.mult),
                        reads=[bpvk], writes=[f"yc{s % 2}_{hd}"])
            if KS in (21, 22):
                continue
            half_t = self.pbt_i % 2
            self.pbt_i += 1
            pk = f"pbt{half_t}"
            pbt = self.pbts[half_t]
            for c4 in range(4):
                S.op("pe", lambda e, c4=c4, s=s, pbt=pbt: e.transpose(pbt[:, c4 * 128:(c4 + 1) * 128],
                                                                          self.yc[:, s % 2, c4 * 128:(c4 + 1) * 128], self.ident_b[:]),
                     reads=[f"yc{s % 2}_{2 * c4}", f"yc{s % 2}_{2 * c4 + 1}", "ident_b"], writes=[pk])
            for c4 in range(4):
                S.op("act" if s % 2 == 0 else "dve", lambda e, s=s, pbt=pbt, c4=c4: e.tensor_scalar(
                    out=self.yT[:, 8 + c4, s * 128:(s + 1) * 128], in0=pbt[:, c4 * 128:(c4 + 1) * 128], scalar1=self.ones_f[:, 0:1], scalar2=None, op0=ALU.mult)
                     if s % 2 else e.activation(out=self.yT[:, 8 + c4, s * 128:(s + 1) * 128], in_=pbt[:, c4 * 128:(c4 + 1) * 128], func=AF.Copy, scale=self.ones_f[:, 0:1]),
                     reads=[pk, "ones_f"], writes=[f"yT{8 + c4}"])

        if KS == 22:
            self.dbg_dump(self.yc[:].rearrange("p a b -> p (a b)"), [f"yc{a}_{hd}" for a in range(2) for hd in range(8)], 0, 1024)
        if KS in (14, 21, 22):
            return
        pp, ybuf, cv = self.pp, self.ybuf, self.cv
        for k in range(31):
            for m in range(4):
                rd = [f"yb{m}", f"ybh{m}", "pp"]
                if k == 0:
                    S.op("dve", lambda e, m=m: e.tensor_scalar(out=cv[:, m, :], in0=ybuf[:, m, 0:512], scalar1=pp[:, PP_CW + m * 31:PP_CW + m * 31 + 1],
                                                               scalar2=pp[:, PP_CB + m:PP_CB + m + 1], op0=ALU.mult, op1=ALU.add), reads=rd, writes=[f"cv{m}"])
                else:
                    S.op("dve", lambda e, m=m, k=k: e.scalar_tensor_tensor(out=cv[:, m, :], in0=ybuf[:, m, k:k + 512], scalar=pp[:, PP_CW + m * 31 + k:PP_CW + m * 31 + k + 1],
                                                                           in1=cv[:, m, :], op0=ALU.mult, op1=ALU.add), reads=rd + [f"cv{m}"], writes=[f"cv{m}"])
        for m in range(4):
            S.op("act", lambda e, m=m: e.copy(out=ybuf[:, m, 0:30], in_=ybuf[:, m, 512:542]), reads=[f"yb{m}"], writes=[f"ybh{m}"])
        b1, b1k = self.bank()
        b2, b2k = self.bank()
        for m in range(4):
            sq, sqk = self.ft()
            S.op("act", lambda e, m=m, sq=sq: e.activation(out=sq[:], in_=cv[:, m, :], func=AF.Square), reads=[f"cv{m}"], writes=[sqk])
            S.op("pe", lambda e, m=m: e.matmul(b1[:], self.ones_f[:], cv[:, m, :], start=(m == 0), stop=(m == 3)), reads=["ones_f", f"cv{m}"], writes=[b1k])
            S.op("pe", lambda e, m=m, sq=sq: e.matmul(b2[:], self.ones_f[:], sq[:], start=(m == 0), stop=(m == 3)), reads=["ones_f", sqk], writes=[b2k])
        mt, mtk = self.ft()
        rs, rsk = self.ft()
        S.op("dve", lambda e: e.tensor_scalar(out=mt[:], in0=b1[:], scalar1=1.0 / 512, scalar2=None, op0=ALU.mult), reads=[b1k], writes=[mtk])
        S.op("dve", lambda e: e.scalar_tensor_tensor(out=rs[:], in0=mt[:], scalar=-1.0, in1=mt[:], op0=ALU.mult, op1=ALU.mult), reads=[mtk], writes=[rsk])
        S.op("dve", lambda e: e.scalar_tensor_tensor(out=rs[:], in0=b2[:], scalar=1.0 / 512, in1=rs[:], op0=ALU.mult, op1=ALU.add), reads=[b2k, rsk], writes=[rsk])
        S.op("act", lambda e: e.activation(out=rs[:], in_=rs[:], func=AF.Ln, bias=EPS), reads=[rsk], writes=[rsk])
        S.op("act", lambda e: e.activation(out=rs[:], in_=rs[:], func=AF.Exp, scale=-0.5), reads=[rsk], writes=[rsk])
        for m in range(4):
            S.op("dve", lambda e, m=m: e.tensor_tensor(out=cv[:, m, :], in0=cv[:, m, :], in1=mt[:], op=ALU.subtract), reads=[f"cv{m}", mtk], writes=[f"cv{m}"])
            S.op("dve", lambda e, m=m: e.tensor_tensor(out=cv[:, m, :], in0=cv[:, m, :], in1=rs[:], op=ALU.mult), reads=[f"cv{m}", rsk], writes=[f"cv{m}"])
            S.op("act", lambda e, m=m: e.activation(out=self.yT[:, 12 + m, :], in_=cv[:, m, :], func=AF.Silu, scale=pp[:, PP_CLG + m:PP_CLG + m + 1],
                                                    bias=pp[:, PP_CLB + m:PP_CLB + m + 1]), reads=[f"cv{m}", "pp"], writes=[f"yT{12 + m}"])

    def merge_tile(self, L):
        S = self.S
        cv, yT, mT, pp = self.cv, self.yT, self.mT, self.pp
        for g2 in range(2):
            for n in range(4):
                wg, wgk = self.load_w(L["w_gate"][n], 0, 8, g2 * 512, g2 * 512 + 512)
                wb, wbk = self.load_w(L["w_branch"][n], 0, 4, g2 * 512, g2 * 512 + 512)
                for mm_ in range(4):
                    m = g2 * 4 + mm_
                    bA, bAk = self.bank()
                    bB, bBk = self.bank()
                    for kc in range(8):
                        S.op("pe", lambda e, bA=bA, kc=kc, mm_=mm_, wg=wg: e.matmul(bA[:], wg[:, kc, mm_ * 128:(mm_ + 1) * 128], self.xnT[:, kc, :],
                                                                                    start=(kc == 0), stop=(kc == 7)), reads=[wgk, f"xnT{kc}"], writes=[bAk])
                    for kc in range(4):
                        rk = [f"yT{n * 4 + kc}"]
                        S.op("pe", lambda e, bB=bB, kc=kc, mm_=mm_, wb=wb, n=n: e.matmul(bB[:], wb[:, kc, mm_ * 128:(mm_ + 1) * 128], yT[:, n * 4 + kc, :],
                                                                                         start=(kc == 0), stop=(kc == 3)), reads=[wbk] + rk, writes=[bBk])
                    gt, gtk = self.ft()
                    S.op("act", lambda e, bA=bA, gt=gt, n=n, m=m: e.activation(out=gt[:], in_=bA[:], func=AF.Sigmoid, bias=pp[:, PP_BG + n * 8 + m:PP_BG + n * 8 + m + 1]),
                         reads=[bAk, "pp"], writes=[gtk])
                    if n == 0:
                        S.op("dve", lambda e, bB=bB, gt=gt, mm_=mm_: e.tensor_tensor(out=cv[:, mm_, :], in0=bB[:], in1=gt[:], op=ALU.mult),
                             reads=[bBk, gtk], writes=[f"cv{mm_}"])
                    else:
                        S.op("dve", lambda e, bB=bB, gt=gt: e.tensor_tensor(out=gt[:], in0=bB[:], in1=gt[:], op=ALU.mult), reads=[bBk, gtk], writes=[gtk])
                        if n < 3:
                            S.op("dve", lambda e, gt=gt, mm_=mm_: e.tensor_tensor(out=cv[:, mm_, :], in0=cv[:, mm_, :], in1=gt[:], op=ALU.add),
                                 reads=[gtk, f"cv{mm_}"], writes=[f"cv{mm_}"])
                        else:
                            S.op("dve", lambda e, gt=gt, mm_=mm_, m=m: e.tensor_tensor(out=mT[:, m, :], in0=cv[:, mm_, :], in1=gt[:], op=ALU.add),
                                 reads=[gtk, f"cv{mm_}"], writes=[f"mT{m}"])
        for g2 in range(2):
            w, wk = self.load_w(L["w_out"], 0, 8, g2 * 512, g2 * 512 + 512)

            def ev(s, b, bk, g2=g2):
                S.op("dve", lambda e: e.tensor_tensor(out=self.h[:, s, g2 * 512:(g2 + 1) * 512], in0=self.h[:, s, g2 * 512:(g2 + 1) * 512], in1=b[:], op=ALU.add),
                     reads=[bk, f"h{s}"], writes=[f"h{s}"])
            self.proj_tok(w, wk, 8, 512, lambda kc, s: mT[:, kc, s * 128:(s + 1) * 128], lambda kc: f"mT{kc}", ev)

    def ffn_tile(self, L):
        S = self.S
        yT = self.yT
        self.norm_T(PP_N2)
        for half in range(2):
            for fg in range(4):
                w, wk = self.load_w(L["w_ff1"], 0, 8, (half * 4 + fg) * 512, (half * 4 + fg) * 512 + 512)

                def ev(m, b, bk, fg=fg):
                    rl, rlk = self.ft()
                    S.op("act", lambda e: e.activation(out=rl[:], in_=b[:], func=AF.Relu), reads=[bk], writes=[rlk])
                    S.op("dve", lambda e: e.tensor_tensor(out=yT[:, fg * 4 + m, :], in0=rl[:], in1=rl[:], op=ALU.mult), reads=[rlk], writes=[f"yT{fg * 4 + m}"])
                self.proj_feat(w, wk, 8, 512, self.xnT_rhs, self.xnT_key, ev)
            for g2 in range(2):
                bsub = [self.bank() for _ in range(4)]
                for piece in range(2):
                    w, wk = self.load_w(L["w_ff2"], half * 16 + piece * 8, 8, g2 * 512, g2 * 512 + 512)
                    for s in range(4):
                        b, bk = bsub[s]
                        for kc in range(8):
                            S.op("pe", lambda e, b=b, kc=kc, s=s, piece=piece, w=w: e.matmul(b[:], yT[:, piece * 8 + kc, s * 128:(s + 1) * 128], w[:, kc, 0:512],
                                                                                             start=(piece == 0 and kc == 0), stop=(piece == 1 and kc == 7)),
                                 reads=[wk, f"yT{piece * 8 + kc}"], writes=[bk])
                for s in range(4):
                    b, bk = bsub[s]
                    S.op("dve", lambda e, b=b, s=s, g2=g2: e.tensor_tensor(out=self.h[:, s, g2 * 512:(g2 + 1) * 512], in0=self.h[:, s, g2 * 512:(g2 + 1) * 512], in1=b[:], op=ALU.add),
                         reads=[bk, f"h{s}"], writes=[f"h{s}"])

    def ple_tile(self, L, p_view, t):
        S = self.S
        self.norm_T(PP_N3)
        S.dma("sp", "ptile", self.ptile[:], p_view[:, t * 4:(t + 1) * 4, :], writes=["ptile"])
        for qc in range(2):
            b, bk = self.bank()
            for s in range(4):
                S.op("pe", lambda e, b=b, s=s, qc=qc: e.transpose(b[:, s * 128:(s + 1) * 128], self.ptile[:, s, qc * 128:(qc + 1) * 128], self.ident_f[:]),
                     reads=["ptile", "ident_f"], writes=[bk])
            S.op("act", lambda e, b=b, qc=qc: e.copy(out=self.pT[:, qc, :], in_=b[:]), reads=[bk], writes=[f"pT{qc}"])
        for g2 in range(2):
            wpg, wpgk = self.load_w(L["w_pg"], 0, 8, g2 * 512, g2 * 512 + 512)
            wpl, wplk = self.load_w(L["w_ple"], 0, 2, g2 * 512, g2 * 512 + 512)
            for s in range(4):
                bA, bAk = self.bank()
                bB, bBk = self.bank()
                for kc in range(8):
                    S.op("pe", lambda e, bA=bA, kc=kc, s=s, wpg=wpg: e.matmul(bA[:], self.xnT[:, kc, s * 128:(s + 1) * 128], wpg[:, kc, 0:512], start=(kc == 0), stop=False),
                         reads=[wpgk, f"xnT{kc}"], writes=[bAk])
                S.op("pe", lambda e, bA=bA, g2=g2: e.matmul(bA[:], self.ones_b[0:1, :], self.bpg[0:1, g2 * 512:(g2 + 1) * 512], start=False, stop=True),
                     reads=["ones_b", "bpg"], writes=[bAk])
                for qc in range(2):
                    S.op("pe", lambda e, bB=bB, qc=qc, s=s, wpl=wpl: e.matmul(bB[:], self.pT[:, qc, s * 128:(s + 1) * 128], wpl[:, qc, 0:512], start=(qc == 0), stop=(qc == 1)),
                         reads=[wplk, f"pT{qc}"], writes=[bBk])
                gt, gtk = self.ft()
                S.op("act", lambda e, bA=bA, gt=gt: e.activation(out=gt[:], in_=bA[:], func=AF.Sigmoid), reads=[bAk], writes=[gtk])
                S.op("dve", lambda e, bB=bB, gt=gt: e.tensor_tensor(out=gt[:], in0=bB[:], in1=gt[:], op=ALU.mult), reads=[bBk, gtk], writes=[gtk])
                S.op("dve", lambda e, gt=gt, s=s, g2=g2: e.tensor_tensor(out=self.h[:, s, g2 * 512:(g2 + 1) * 512], in0=self.h[:, s, g2 * 512:(g2 + 1) * 512], in1=gt[:], op=ALU.add),
                     reads=[gtk, f"h{s}"], writes=[f"h{s}"])

    def final_norm_store(self, nout_view, t):
        S = self.S
        for s in range(4):
            ss, kss = self.sm()
            jk, jkk = self.bt()
            S.op("dve", lambda e, ss=ss: e.memset(ss, 0.0), writes=[kss])
            S.op("act", lambda e, s=s, ss=ss, jk=jk: e.activation(out=jk[:], in_=self.h[:, s, 0:512], func=AF.Square, accum_out=ss), reads=[f"h{s}"], writes=[jkk, kss])
            s2, ks2 = self.sm()
            jk2, jk2k = self.bt()
            S.op("dve", lambda e, s2=s2: e.memset(s2, 0.0), writes=[ks2])
            S.op("act", lambda e, s=s, s2=s2, jk2=jk2: e.activation(out=jk2[:], in_=self.h[:, s, 512:1024], func=AF.Square, accum_out=s2), reads=[f"h{s}"], writes=[jk2k, ks2])
            S.op("dve", lambda e, ss=ss, s2=s2: e.tensor_tensor(out=ss, in0=ss, in1=s2, op=ALU.add), reads=[kss, ks2], writes=[kss])
            S.op("act", lambda e, ss=ss: e.activation(out=ss, in_=ss, func=AF.Ln, scale=1.0 / D, bias=EPS), reads=[kss], writes=[kss])
            S.op("act", lambda e, ss=ss: e.activation(out=ss, in_=ss, func=AF.Exp, scale=-0.5), reads=[kss], writes=[kss])
            for hf in range(2):
                o, ok = self.ft()
                S.op("dve", lambda e, o=o, s=s, ss=ss, hf=hf: e.scalar_tensor_tensor(out=o[:], in0=self.h[:, s, hf * 512:(hf + 1) * 512], scalar=ss,
                                                                                  in1=self.fg_bc[:, hf * 512:(hf + 1) * 512], op0=ALU.mult, op1=ALU.mult),
                     reads=[f"h{s}", kss, "fg_bc"], writes=[ok])
                S.dma("sp", f"nout{s}{hf}", nout_view[:, t * 4 + s, hf * 512:(hf + 1) * 512], o[:], reads=[ok], writes=["nout"])

    def dbg_dump(self, ap, key, col0, n):
        S = self.S
        for c in range(0, n, 512):
            w = min(512, n - c)
            f, fk = self.ft()
            S.op("dve", lambda e, f=f, c=c, w=w: e.tensor_copy(out=f[:, 0:w], in_=ap[:, c:c + w]), reads=[key] if isinstance(key, str) else key, writes=[fk])
            S.dma("sp", None, self.dbg[:, col0 + c:col0 + c + w], f[:, 0:w], reads=[fk], writes=["dbg"])

    def load_h(self, view, t):
        for s in range(4):
            self.S.dma("sp", f"hld{s}", self.h[:, s, :], view[:, t * 4 + s, :], writes=[f"h{s}"])


def build_summary(NT):
    P = Prog("summary", NT)
    S = P.S
    hin = P.din("hin", [NT * T, D]).rearrange("(n p) d -> p n d", p=128)
    oF = P.dout("oF", [128, 256])
    oD = P.dout("oD", [128, 2])
    P.setup_common()
    P.setup_summary_bufs()
    L = P.declare_layer("", False)
    P.load_layer_params(L, False)
    S.op("dve", lambda e: e.memset(P.Sst[:], 0.0), writes=["Sst00", "Sst01", "Sst10", "Sst11"])
    S.op("dve", lambda e: e.memset(P.dlog[:], 0.0), writes=["dlog"])
    W = L["w_in"]
    for t in range(NT):
        P.load_h(hin, t)
        P.norm_T(PP_N1)
        w, wk = P.load_w(W, 0, 8, 1280, 1536)
        P.proj_tok(w, wk, 8, 256, P.xnT_lhs, P.xnT_key, lambda s, b, bk: S.op(
            "dve", lambda e: e.tensor_copy(out=P.kf[:, s, :], in_=b[:, 0:256]), reads=[bk], writes=[f"kf{s}"]))
        w, wk = P.load_w(W, 0, 8, 1536, 2048)
        P.proj_tok(w, wk, 8, 512, P.xnT_lhs, P.xnT_key, lambda s, b, bk: S.op(
            "act", lambda e: e.copy(out=P.vg[:, s, :], in_=b[:]), reads=[bk], writes=[f"vg{s}"]))
        w, wk = P.load_w(W, 0, 8, 2560, 2576)
        P.proj_feat(w, wk, 8, 16, P.xnT_rhs, P.xnT_key, lambda m, b, bk: S.op(
            "dve", lambda e: e.tensor_copy(out=P.aT[0:16, :], in_=b[0:16, :]), reads=[bk], writes=["aT"]))
        P.gla_tile(False)
    dd, ddk = P.sm(2)
    S.op("act", lambda e: e.activation(out=dd, in_=P.dlog[:], func=AF.Exp), reads=["dlog"], writes=[ddk])
    S.dma("sp", "oD", oD, dd, reads=[ddk], writes=["oD"])
    S.dma("sp", "oF", oF, P.Sst[:].rearrange("p a e -> p (a e)"), reads=["Sst00", "Sst01", "Sst10", "Sst11"], writes=["oF"])
    S.finish_wait("sp", ["oD", "oF"])
    S.emit()
    return P.nc


def build_main(NT):
    P = Prog("main", NT)
    S = P.S
    hin = P.din("hin", [NT * T, D]).rearrange("(n p) d -> p n d", p=128)
    halo = P.din("halo", [T, D]).rearrange("(n p) d -> p n d", p=128)
    pin = P.din("pin", [NT * T, 256]).rearrange("(n p) q -> p n q", p=128)
    flag = P.din("flag", [128, 1])
    sumF = P.din("sumF", [3, 128, 256])
    sumD = P.din("sumD", [3, 128, 2])
    fg = P.din("final_g", [D])
    hout = P.dout("hout", [NT * T, D]).rearrange("(n p) d -> p n d", p=128)
    nout = P.dout("nout", [NT * T, D]).rearrange("(n p) d -> p n d", p=128)
    import os
    if os.environ.get("KSTOP"):
        P.dbg = P.dout("dbg", [128, 2048])
    P.setup_common()
    P.setup_full()
    L = P.declare_layer("", True)
    P.load_layer_params(L, True)
    S.dma("sp", None, P.flag[:], flag, writes=["flag"])
    S.dma("sp", None, P.fg_bc[:], fg.partition_broadcast(128), writes=["fg_bc"])
    skeys = ["Sst00", "Sst01", "Sst10", "Sst11"]
    S.op("dve", lambda e: e.memset(P.Sst[:], 0.0), writes=skeys)
    for sl in range(3):
        ftl, fk = P.ft()
        dtl, dk = P.sm(2)
        S.dma("sp", None, ftl[:, 0:256], sumF[sl], writes=[fk])
        S.dma("sp", None, dtl, sumD[sl], writes=[dk])
        for hp in range(2):
            S.op("dve", lambda e, ftl=ftl, dtl=dtl, hp=hp: e.scalar_tensor_tensor(out=P.Sst[:, hp, :], in0=P.Sst[:, hp, :], scalar=dtl[:, hp:hp + 1],
                                                                                 in1=ftl[:, hp * 128:(hp + 1) * 128], op0=ALU.mult, op1=ALU.add),
                 reads=[fk, dk, f"Sst{hp}0", f"Sst{hp}1"], writes=[f"Sst{hp}0", f"Sst{hp}1"])
    import os
    KSTOP = int(os.environ.get("KSTOP", "99"))
    P.load_h(halo, 0)
    P.norm_T(PP_N1)
    P.mixer_tile(L, 0, True)
    for t in range(NT):
        if KSTOP <= 1:
            break
        P.load_h(hin, t)
        P.norm_T(PP_N1)
        P.mixer_tile(L, t, False)
        if KSTOP <= 2 or KSTOP in (11, 12, 13, 14, 21, 22):
            break
        P.merge_tile(L)
        if KSTOP <= 3:
            break
        P.ffn_tile(L)
        if KSTOP <= 4:
            break
        P.ple_tile(L, pin, t)
        for s in range(4):
            S.dma("sp", f"hout{s}", hout[:, t * 4 + s, :], P.h[:, s, :], reads=[f"h{s}"], writes=["hout"])
        P.final_norm_store(nout, t)
    S.finish_wait("sp", ["hout", "nout", "dbg"])
    S.emit()
    return P.nc


def _consts():
    ident = np.eye(128, dtype=np.float32)
    lp = np.arange(128)
    same = (lp[:, None] // 64) == (lp[None, :] // 64)
    T2 = np.where(same & (lp[:, None] > lp[None, :]), np.float32(-1.0 / 16), np.float32(0)).astype(np.float32)
    cind = np.zeros((128, 2), np.float32)
    cind[:64, 0] = -1.0 / 16
    cind[64:, 1] = -1.0 / 16
    return {"c_ident": ident, "c_T2": T2, "c_cind": cind}


def _bias_layout(table):
    m = np.arange(128)[:, None, None]
    j = np.arange(5)[None, :, None]
    l = np.arange(128)[None, None, :]
    kpos = 128 * j + m
    qc = 8 + l // 64
    kc = kpos // 64
    valid = ((qc - kc) >= 0) & ((qc - kc) <= 8)
    rel = 512 + l - kpos
    idx = np.clip(rel, -63, 256) + 63
    g = table[:, idx]
    g = np.where(valid[None], g, np.float32(NEGB)).astype(np.float32)
    g = g[[0, 2, 1, 3, 4, 6, 5, 7]]
    return np.ascontiguousarray(g.transpose(1, 2, 0, 3)).reshape(128, 5 * 8 * 128)


def _layer_inputs(inp, i):
    f = lambda a: np.ascontiguousarray(a, dtype=np.float32)
    d = {}
    d["w_in"] = f(inp["w_in"][i])
    pp = np.zeros((128, NPP), np.float32)
    pp[:, PP_N1:PP_N1 + 8] = inp["norm1_g"][i].reshape(8, 128).T
    pp[:, PP_N2:PP_N2 + 8] = inp["norm2_g"][i].reshape(8, 128).T
    pp[:, PP_N3:PP_N3 + 8] = inp["norm3_g"][i].reshape(8, 128).T
    pp[:, PP_GNG:PP_GNG + 4] = inp["gla_norm_g"][i].reshape(4, 128).T
    pp[:, PP_CB:PP_CB + 4] = inp["conv_dw_b"][i].reshape(4, 128).T
    pp[:, PP_CLG:PP_CLG + 4] = inp["conv_ln_g"][i].reshape(4, 128).T
    pp[:, PP_CLB:PP_CLB + 4] = inp["conv_ln_b"][i].reshape(4, 128).T
    pp[:, PP_BG:PP_BG + 32] = inp["b_gate"][i].reshape(4, 8, 128).transpose(2, 0, 1).reshape(128, 32)
    pp[:, PP_CW:PP_CW + 124] = inp["conv_dw_w"][i].reshape(31, 4, 128).transpose(2, 1, 0).reshape(128, 124)
    d["pp"] = pp
    d["w_a2"] = f(inp["gla_w_a2"][i])
    d["b_a"] = f(inp["gla_b_a"][i])
    for n in range(4):
        d[f"w_gate{n}"] = f(inp["w_gate"][i, n])
        d[f"w_branch{n}"] = f(inp["w_branch"][i, n])
    d["w_out"] = f(inp["w_out"][i])
    d["w_ff1"] = f(inp["w_ff1"][i])
    d["w_ff2"] = f(inp["w_ff2"][i])
    d["w_pg"] = f(inp["w_ple_gate"][i])
    d["w_ple"] = f(inp["w_ple"][i])
    d["b_pg"] = f(inp["b_ple_gate"][i]).reshape(1, D)
    d["sg_lng"] = f(inp["sg_ln_g"][i])
    d["sg_lnb"] = f(inp["sg_ln_b"][i])
    d["sg_wT"] = f(np.transpose(inp["sg_w"][i], (2, 0, 1)))
    d["sg_b"] = f(inp["sg_b"][i]).reshape(512)
    d["biasT"] = _bias_layout(np.asarray(inp["att_rel_bias"][i], np.float32))
    return d


SUMMARY_KEYS = ["w_in", "pp", "w_a2", "b_a"]
_CACHE = {}


def _get(kind, NT):
    k = (kind, NT)
    if k not in _CACHE:
        _CACHE[k] = build_summary(NT) if kind == "summary" else build_main(NT)
    return _CACHE[k]


def run_model(inp, n_cores):
    x = np.asarray(inp["x"], np.float32)
    p = np.asarray(inp["p"], np.float32)
    B, SEQ, _ = x.shape
    depth = p.shape[0]
    cpb = n_cores // B
    TOK = SEQ // cpb
    NT = TOK // T
    consts = _consts()
    cores = [(b, k) for b in range(B) for k in range(cpb)]
    hcur = [np.ascontiguousarray(x[b, k * TOK:(k + 1) * TOK]) for (b, k) in cores]
    nout = None
    for i in range(depth):
        Ld = _layer_inputs(inp, i)
        ncs = _get("summary", NT)
        maps = []
        for ci in range(n_cores):
            m = {"hin": hcur[ci]}
            m.update({k: Ld[k] for k in SUMMARY_KEYS})
            m.update(consts)
            maps.append(m)
        res = run_bass_kernel_spmd(ncs, maps, core_ids=list(range(n_cores))).results
        oF = [np.asarray(r["oF"]) for r in res]
        oD = [np.asarray(r["oD"]) for r in res]
        ncm = _get("main", NT)
        maps = []
        for ci, (b, k) in enumerate(cores):
            m = {"hin": hcur[ci], "pin": np.ascontiguousarray(p[i, b, k * TOK:(k + 1) * TOK])}
            if k == 0:
                m["halo"] = np.zeros((T, D), np.float32)
                m["flag"] = np.zeros((128, 1), np.float32)
            else:
                m["halo"] = np.ascontiguousarray(hcur[ci - 1][TOK - T:])
                m["flag"] = np.ones((128, 1), np.float32)
            sF = np.zeros((3, 128, 256), np.float32)
            sD = np.zeros((3, 128, 2), np.float32)
            for jj in range(k):
                sF[3 - k + jj] = oF[ci - k + jj]
                sD[3 - k + jj] = oD[ci - k + jj]
            m["sumF"], m["sumD"] = sF, sD
            m["final_g"] = np.asarray(inp["final_g"], np.float32)
            m.update(Ld)
            m.update(consts)
            maps.append(m)
        res = run_bass_kernel_spmd(ncm, maps, core_ids=list(range(n_cores))).results
        global _LAST
        _LAST = res
        hcur = [np.asarray(r["hout"]) for r in res]
        nout = [np.asarray(r["nout"]) for r in res]
    out = np.empty((B, SEQ, D), np.float32)
    for ci, (b, k) in enumerate(cores):
        out[b, k * TOK:(k + 1) * TOK] = nout[ci]
    return out


def kernel(**inputs):
    return run_model(inputs, 8)
```
